# Optimizing a Trainium2 kernel written in Bass

```python
import math
import jax, jax.numpy as jnp
from jax import lax
import numpy as np

D_MODEL = 4096
BATCH = 2
SEQ = 4096
DEPTH = 2

N_BRANCH = 3
CHUNK = 128
HD = 128
D_A = D_MODEL // 2
H_A = D_A // HD
D_B = D_MODEL // 2
SC_K = 3
D_C = D_MODEL // 2
CF_K = 31
D_FF = 11008
FF_K = 3
ALPHA = (2.0 * DEPTH) ** 0.25
BETA = (8.0 * DEPTH) ** -0.25
LN_EPS = 1e-5

OFF_A = 0
OFF_B = OFF_A + 2 * D_A
OFF_C = OFF_B + 3 * D_B
OFF_G = OFF_C + 2 * D_C
N_IN = OFF_G + N_BRANCH * D_MODEL

kernel_name = "hybrid_gated_sgu_shortconv_conformer_deepnorm"


def layer_norm(x, g, b):
    xf = x.astype(jnp.float32)
    mu = jnp.mean(xf, axis=-1, keepdims=True)
    var = jnp.mean(jnp.square(xf - mu), axis=-1, keepdims=True)
    y = (xf - mu) * lax.rsqrt(var + LN_EPS)
    return (y * g.astype(jnp.float32) + b.astype(jnp.float32)).astype(x.dtype)


def causal_dwconv(x, w, b=None):
    k = w.shape[0]
    y = lax.conv_general_dilated(
        x, w[:, None, :].astype(x.dtype), window_strides=(1,), padding=[(k - 1, 0)],
        dimension_numbers=("NWC", "WIO", "NWC"), feature_group_count=x.shape[-1])
    return y if b is None else y + b


def chunked_sgu(z, ln_g, ln_b, ws, bs):
    bsz, s, _ = z.shape
    n = s // CHUNK
    u, v = z[..., :D_A], z[..., D_A:]
    v = layer_norm(v.reshape(bsz, s, H_A, HD), ln_g.reshape(H_A, HD), ln_b.reshape(H_A, HD))
    v = v.reshape(bsz, n, CHUNK, H_A, HD)
    mask = jnp.tril(jnp.ones((CHUNK, CHUNK), dtype=bool))
    w = jnp.where(mask[None], ws, 0).astype(v.dtype)
    mixed = jnp.einsum("hts,bnshc->bnthc", w, v) + jnp.swapaxes(bs, 0, 1)[:, :, None]
    return (u.reshape(bsz, n, CHUNK, H_A, HD) * mixed).reshape(bsz, s, D_A)


def hybrid_mixer(x, w_in, b_gate, a_ln_g, a_ln_b, a_ws, a_bs, a_out, b_conv, b_out,
                 c_conv, c_conv_b, c_ln_g, c_ln_b, c_out, w_o):
    bsz, s, d = x.shape
    z = jnp.einsum("bsd,dn->bsn", x, w_in)
    za = jax.nn.gelu(z[..., OFF_A:OFF_B])
    y_a = jnp.einsum("bsc,cd->bsd", chunked_sgu(za, a_ln_g, a_ln_b, a_ws, a_bs), a_out)
    gb = z[..., OFF_B:OFF_B + D_B]
    gc = z[..., OFF_B + D_B:OFF_B + 2 * D_B]
    hb = z[..., OFF_B + 2 * D_B:OFF_C]
    y_b = jnp.einsum("bsc,cd->bsd", gb * causal_dwconv(gc * hb, b_conv), b_out)
    ca = z[..., OFF_C:OFF_C + D_C] * jax.nn.sigmoid(z[..., OFF_C + D_C:OFF_G])
    cc = jax.nn.silu(layer_norm(causal_dwconv(ca, c_conv, c_conv_b), c_ln_g, c_ln_b))
    y_c = jnp.einsum("bsc,cd->bsd", cc, c_out)
    g = jax.nn.sigmoid(z[..., OFF_G:].reshape(bsz, s, N_BRANCH, d) + b_gate)
    m = g[:, :, 0] * y_a + g[:, :, 1] * y_b + g[:, :, 2] * y_c
    return jnp.einsum("bsd,de->bse", m, w_o)


def conv_ffn(x, f_up, f_conv, f_down):
    h = causal_dwconv(jnp.einsum("bsd,df->bsf", x, f_up), f_conv)
    return jnp.einsum("bsf,fd->bsd", jax.nn.silu(h[..., :D_FF]) * h[..., D_FF:], f_down)


def setup_inputs(seed: int = 0) -> dict:
    key = jax.random.key(seed)
    ks = jax.random.split(key, 24)
    L, D = DEPTH, D_MODEL

    def nrm(k, shape, scale):
        return jax.random.normal(k, shape, jnp.float32) * scale

    return {
        "x": nrm(ks[0], (BATCH, SEQ, D), 1.0),
        "w_in": nrm(ks[1], (L, D, N_IN), D ** -0.5),
        "b_gate": nrm(ks[2], (L, N_BRANCH, D), 0.02),
        "a_ln_g": 1.0 + nrm(ks[3], (L, D_A), 0.02),
        "a_ln_b": nrm(ks[4], (L, D_A), 0.02),
        "a_ws": nrm(ks[5], (L, H_A, CHUNK, CHUNK), CHUNK ** -0.5),
        "a_bs": 1.0 + nrm(ks[6], (L, H_A, CHUNK), 0.02),
        "a_out": nrm(ks[7], (L, D_A, D), D_A ** -0.5 * BETA),
        "b_conv": nrm(ks[8], (L, SC_K, D_B), SC_K ** -0.5),
        "b_out": nrm(ks[9], (L, D_B, D), D_B ** -0.5 * BETA),
        "c_conv": nrm(ks[10], (L, CF_K, D_C), CF_K ** -0.5),
        "c_conv_b": nrm(ks[11], (L, D_C), 0.02),
        "c_ln_g": 1.0 + nrm(ks[12], (L, D_C), 0.02),
        "c_ln_b": nrm(ks[13], (L, D_C), 0.02),
        "c_out": nrm(ks[14], (L, D_C, D), D_C ** -0.5 * BETA),
        "w_o": nrm(ks[15], (L, D, D), D ** -0.5 * BETA),
        "ln1_g": 1.0 + nrm(ks[16], (L, D), 0.02),
        "ln1_b": nrm(ks[17], (L, D), 0.02),
        "f_up": nrm(ks[18], (L, D, 2 * D_FF), D ** -0.5 * BETA),
        "f_conv": nrm(ks[19], (L, FF_K, 2 * D_FF), FF_K ** -0.5),
        "f_down": nrm(ks[20], (L, D_FF, D), D_FF ** -0.5 * BETA),
        "ln2_g": 1.0 + nrm(ks[21], (L, D), 0.02),
        "ln2_b": nrm(ks[22], (L, D), 0.02),
    }


def reference(x, w_in, b_gate, a_ln_g, a_ln_b, a_ws, a_bs, a_out, b_conv, b_out,
              c_conv, c_conv_b, c_ln_g, c_ln_b, c_out, w_o, ln1_g, ln1_b,
              f_up, f_conv, f_down, ln2_g, ln2_b):
    for l in range(DEPTH):
        mix = hybrid_mixer(x, w_in[l], b_gate[l], a_ln_g[l], a_ln_b[l], a_ws[l], a_bs[l], a_out[l],
                           b_conv[l], b_out[l], c_conv[l], c_conv_b[l], c_ln_g[l], c_ln_b[l],
                           c_out[l], w_o[l])
        x = layer_norm(ALPHA * x + mix, ln1_g[l], ln1_b[l])
        x = layer_norm(ALPHA * x + conv_ffn(x, f_up[l], f_conv[l], f_down[l]), ln2_g[l], ln2_b[l])
    return x
```

```python
import numpy as np
from contextlib import ExitStack
import concourse.bass as bass
import concourse.mybir as mybir
from concourse.bass_utils import run_bass_kernel_spmd

F32 = mybir.dt.float32
BF16 = mybir.dt.bfloat16
AF = mybir.ActivationFunctionType
ALU = mybir.AluOpType

D = 4096
DEPTH = 2
DA = 2048
DFF = 11008
NFC = 86
OFF_A, OFF_B, OFF_C, OFF_G = 0, 4096, 10240, 14336
ALPHA = (2.0 * DEPTH) ** 0.25
LN_EPS = 1e-5
NWIN = 1280
HALO = 256
NOUT = 1024
SLAB = 8192
NSLOT = 3
NCORES = 8

S1_SLABS = 56
PER_LAYER = 56 * 8192 + 32 * (6144 + 8192 + 4096) + 16 * 8192 + 86 * 8192 + 64 * 5504

PO = {}
_o = 0
for _n, _sz in (("bg", 96), ("alg", 16), ("alb", 16), ("bcv", 48), ("ccv", 496), ("ccb", 16), ("clg", 16),
                ("clb", 16), ("l1g", 32), ("l1b", 32), ("l2g", 32), ("l2b", 32), ("fcv", 516)):
    PO[_n] = _o
    _o += _sz
NPL = _o
NPAR = NPL * DEPTH


def geom(L, st):
    if st == 0:
        if L == 0:
            return dict(base=0, v0=0, c0=94, a=126, f0=128, b=512)
        return dict(base=0, v0=128, c0=222, a=254, f0=256, b=512)
    s0 = 512 + 384 * (st - 1)
    return dict(base=s0, v0=s0, c0=s0, a=s0, f0=s0, b=s0 + 384)


class Tracker:
    ENGS = ("tensor", "vector", "scalar", "gpsimd", "sync")

    def __init__(self, eng_sems):
        self.eng_sem = eng_sems
        self.cnt = {k: 0 for k in eng_sems}
        self.seen = {k: {} for k in self.ENGS}
        self.prog = {k: [] for k in self.ENGS}
        self.res = {}
        self.dma_tot = {}
        self.n_wait = 0
        self.n_ins = 0

    def _deps(self, reads, writes):
        deps = []
        for r in reads:
            st = self.res.get(r)
            if st and st[0] is not None:
                deps.append(st[0])
        for w in writes:
            st = self.res.get(w)
            if st:
                if st[0] is not None:
                    deps.append(st[0])
                deps.extend(st[1].values())
        return deps

    def _wait(self, eng, deps):
        seen = self.seen[eng]
        best = {}
        for (sem, val) in deps:
            k = id(sem)
            if seen.get(k, 0) >= val:
                continue
            if k not in best or best[k][1] < val:
                best[k] = (sem, val)
        for k, (sem, val) in best.items():
            self.prog[eng].append(("wait", sem, val))
            seen[k] = val
            self.n_wait += 1

    def _record(self, ev, reads, writes):
        k = id(ev[0])
        for r in reads:
            st = self.res.get(r)
            if st is None:
                st = [None, {}]
                self.res[r] = st
            old = st[1].get(k)
            if old is None or old[1] < ev[1]:
                st[1][k] = ev
        for w in writes:
            self.res[w] = [ev, {}]

    def op(self, eng, fn, reads=(), writes=()):
        return self.group(eng, [fn], reads, writes)

    def group(self, eng, fns, reads=(), writes=()):
        self._wait(eng, self._deps(reads, writes))
        sem = self.eng_sem[eng]
        self.cnt[eng] += 1
        ev = (sem, self.cnt[eng])
        for fn in fns[:-1]:
            self.prog[eng].append(("ins", fn, None, 0))
        self.prog[eng].append(("ins", fns[-1], sem, 1))
        self.n_ins += len(fns)
        self._record(ev, reads, writes)
        return ev

    def dma(self, queue, sem, fn, reads=(), writes=()):
        self._wait(queue, self._deps(reads, writes))
        k = id(sem)
        self.dma_tot[k] = self.dma_tot.get(k, 0) + 16
        ev = (sem, self.dma_tot[k])
        self.prog[queue].append(("ins", fn, sem, 16))
        self.n_ins += 1
        self._record(ev, reads, writes)
        return ev

    def wait_all(self, eng, resources):
        deps = []
        for r in resources:
            st = self.res.get(r)
            if st:
                if st[0] is not None:
                    deps.append(st[0])
                deps.extend(st[1].values())
        self._wait(eng, deps)

    def replay(self, eng, e):
        for it in self.prog[eng]:
            if it[0] == "wait":
                e.wait_ge(it[1], it[2])
            else:
                ins = it[1](e)
                if it[2] is not None:
                    ins.then_inc(it[2], it[3])


class Builder:
    def __init__(self, n_st=3, n_layers=DEPTH, dbg=None):
        self.n_st = n_st
        self.n_layers = n_layers
        self.dbg = dbg
        self.nc = bass.Bass("TRN2", target_bir_lowering=False)
        self.es = ExitStack()

    def sb(self, name, shape, dt):
        return self.es.enter_context(self.nc.sbuf_tensor(name, shape, dt))

    def sem(self, name):
        return self.es.enter_context(self.nc.semaphore(name))

    def build(self):
        nc = self.nc
        es = self.es
        with es:
            self._build()
        return nc

    def _build(self):
        nc = self.nc
        self.xw = nc.dram_tensor("xw", [D, NWIN], F32, kind="ExternalInput").ap()
        self.wst = nc.dram_tensor("wst", [128, DEPTH * PER_LAYER], F32, kind="ExternalInput").ap()
        self.par_d = nc.dram_tensor("par", [128, NPAR], F32, kind="ExternalInput").ap()
        self.awst_d = nc.dram_tensor("awst", [128, DEPTH * 2048], F32, kind="ExternalInput").ap()
        self.cst_d = nc.dram_tensor("cst", [128, 256], F32, kind="ExternalInput").ap()
        self.abs_d = nc.dram_tensor("absr", [1, DEPTH * 2048], F32, kind="ExternalInput").ap()
        self.hm_d = nc.dram_tensor("hmask", [128, 1], F32, kind="ExternalInput").ap()
        self.out_d = nc.dram_tensor("out", [D, NOUT], F32, kind="ExternalOutput").ap()
        self.rscr = nc.dram_tensor("rscr", [32, 128, 388], F32).ap()
        self.xmscr = nc.dram_tensor("xmscr", [32, 128, 388], F32).ap()
        self.x1scr = nc.dram_tensor("x1scr", [32, 128, NWIN], F32).ap()
        if self.dbg:
            self.dbg_d = nc.dram_tensor("dbg", list(self.dbg["shape"]), F32, kind="ExternalOutput").ap()

        self.xT = self.sb("xT", [128, 32, 512], BF16)
        self.region = self.sb("region", [128, 33024], BF16)
        self.ring = [self.sb(f"ring{i}", [128, SLAB], BF16) for i in range(NSLOT)]
        self.par = self.sb("par_sb", [128, NPAR], F32)
        self.WT = self.sb("WT", [128, DEPTH * 16 * 128], BF16)
        self.cmask = self.sb("cmask", [128, 128], F32)
        self.ident = self.sb("ident", [128, 128], BF16)
        self.onesV = self.sb("onesV", [128, 128], F32)
        self.onesC = self.sb("onesC", [128, 128], F32)
        self.onesD = self.sb("onesD", [128, 128], F32)
        self.onesrow = self.sb("onesrow", [1, 128], F32)
        self.bsr = [self.sb(f"bsr{i}", [1, 128], F32) for i in range(2)]
        self.bs_i = 0
        self.hmask = self.sb("hmask_sb", [128, 1], F32)
        self.qtail = self.sb("qtail", [128, DEPTH * 16 * 2], F32)
        self.catail = self.sb("catail", [128, DEPTH * 16 * 30], F32)
        self.utail = self.sb("utail", [128, DEPTH * 172 * 2], F32)
        NTMP = 11
        self.tmp = [self.sb(f"tmp{i}", [128, 512 if i < 4 else 448], F32) for i in range(NTMP)]
        self.tmpb = [self.sb(f"tmpb{i}", [128, 512], BF16) for i in range(2)]
        R = self.region
        n16 = 16 * 386
        self.ug = R[:, 0:n16].rearrange("p (c n) -> p c n", c=16)
        self.bbr = R[:, n16:2 * n16].rearrange("p (c n) -> p c n", c=16)
        self.cc = R[:, 2 * n16:3 * n16].rearrange("p (c n) -> p c n", c=16)
        self.convC = R[:, 3 * n16:5 * n16].bitcast(F32).rearrange("p (c n) -> p c n", c=16)
        self.mbuf = R[:, 3 * n16:5 * n16].rearrange("p (c n) -> p c n", c=32)
        self.act = R[:, 0:NFC * 384].rearrange("p (c n) -> p c n", c=NFC)

        self.psm = [self.es.enter_context(nc.psum_tensor(f"psm{i}", [128, 512], F32)) for i in range(4)]
        self.psS1 = self.es.enter_context(nc.psum_tensor("psS1", [128, 512], F32))
        self.psS2 = self.es.enter_context(nc.psum_tensor("psS2", [128, 512], F32))
        self.psG = self.es.enter_context(nc.psum_tensor("psG", [128, 512], F32))
        self.psX = self.es.enter_context(nc.psum_tensor("psX", [128, 512], F32))
        self.psT = self.psX[:, 0:256].bitcast(BF16)
        self.psm_i = 0

        eng_sems = {k: self.sem(f"s_{k}") for k in ("tensor", "vector", "scalar", "gpsimd")}
        self.T = Tracker(eng_sems)
        self.s_w = [self.sem(f"s_w{i}") for i in range(NSLOT)]
        self.s_init = self.sem("s_init")
        self.s_bs = [self.sem(f"s_bs{i}") for i in range(2)]
        self.s_x = self.sem("s_x")
        self.NLD = 3
        self.s_ld = [self.sem(f"s_ld{i}") for i in range(self.NLD)]
        self.ldbuf = [self.sb(f"ldbuf{i}", [128, 388], F32) for i in range(self.NLD)]
        self.ld_i = 0
        self.NSTB = 3
        self.s_st = [self.sem(f"s_st{i}") for i in range(self.NSTB)]
        self.stbuf = [self.sb(f"stbuf{i}", [128, 388], F32) for i in range(self.NSTB)]
        self.st_i = 0
        self.s_out = self.sem("s_out")
        self.slab_i = 0
        self.final_res = []

        self.emit_init()
        for st in range(self.n_st):
            for L in range(self.n_layers):
                self.emit_layer(L, st)
        T = self.T
        T.wait_all("sync", self.final_res)
        print("sbuf bytes remaining", nc.sbuf_bytes_remaining, "instructions", T.n_ins, "waits", T.n_wait,
              "slabs", self.slab_i, flush=True)
        with nc.Block() as block:
            @block.sync
            def _(e):
                T.replay("sync", e)

            @block.gpsimd
            def _(e):
                T.replay("gpsimd", e)

            @block.tensor
            def _(e):
                T.replay("tensor", e)

            @block.scalar
            def _(e):
                T.replay("scalar", e)

            @block.vector
            def _(e):
                T.replay("vector", e)

    def pidx(self, L, name, i=0):
        o = L * NPL + PO[name] + i
        return self.par[:, o:o + 1]

    def next_ps(self):
        b = self.psm_i % 4
        self.psm_i += 1
        return b

    def load_slab(self, L, off, n):
        slot = self.slab_i % NSLOT
        self.slab_i += 1
        src = self.wst[:, L * PER_LAYER + off: L * PER_LAYER + off + n]
        dst = self.ring[slot][:, 0:n]
        self.T.dma("gpsimd", self.s_w[slot], lambda e, dst=dst, src=src: e.dma_start(out=dst, in_=src),
                   writes=[("ring", slot)])
        return slot, dst

    def mm_group(self, ps_ap, pairs, reads, writes, first=True, last=True):
        n = len(pairs)
        fns = []
        for i, (l, r) in enumerate(pairs):
            fns.append(lambda e, l=l, r=r, i=i: e.matmul(ps_ap, l, r, start=(first and i == 0),
                                                         stop=(last and i == n - 1)))
        return self.T.group("tensor", fns, reads=reads, writes=writes)

    def emit_init(self):
        T = self.T
        si = self.s_init
        T.dma("sync", si, lambda e: e.dma_start(out=self.par[:], in_=self.par_d), writes=["par"])
        T.dma("sync", si, lambda e: e.dma_start(out=self.cmask[:], in_=self.cst_d[:, 0:128]), writes=["cmask"])
        T.dma("gpsimd", si, lambda e: e.dma_start(out=self.ident[:], in_=self.cst_d[:, 128:256]), writes=["ident"])
        T.dma("sync", si, lambda e: e.dma_start(out=self.hmask[:], in_=self.hm_d), writes=["hmask"])
        T.op("vector", lambda e: e.memset(self.onesV[:], 1.0 / 128), writes=["onesV"])
        T.op("vector", lambda e: e.memset(self.onesC[:], 1.0 / 2048), writes=["onesC"])
        T.op("vector", lambda e: e.memset(self.onesD[:], 1.0 / 4096), writes=["onesD"])
        T.op("vector", lambda e: e.memset(self.onesrow[:], 1.0), writes=["onesrow"])
        T.op("vector", lambda e: e.memset(self.qtail[:], 0.0), writes=["qtail"])
        T.op("vector", lambda e: e.memset(self.catail[:], 0.0), writes=["catail"])
        T.op("vector", lambda e: e.memset(self.utail[:], 0.0), writes=["utail"])
        for L in range(DEPTH):
            stage = self.region[:, 0:4096].bitcast(F32)
            T.dma("sync", si, lambda e, L=L, stage=stage: e.dma_start(
                out=stage, in_=self.awst_d[:, L * 2048:(L + 1) * 2048]), writes=["awstage"])
            for h in range(16):
                o = (L * 16 + h) * 128
                T.op("vector", lambda e, o=o, h=h, stage=stage: e.tensor_tensor(
                    out=self.WT[:, o:o + 128], in0=stage[:, h * 128:(h + 1) * 128], in1=self.cmask[:],
                    op=ALU.mult), reads=["awstage", "cmask"], writes=[("WT", L, h)])
        self.region_init_res = ["awstage"]

    def emit_layer(self, L, st):
        T = self.T
        g = geom(L, st)
        base, v0, c0, a, f0, b = g["base"], g["v0"], g["c0"], g["a"], g["f0"], g["b"]
        Nv, Nc, Nm, Nf = b - v0, b - c0, b - a, b - f0
        last_layer = (L == self.n_layers - 1)
        xT = self.xT
        tmp = self.tmp
        XT_ALL = [("xT", k) for k in range(32)]
        off = 0

        if L == 0:
            src = self.xw.rearrange("(kc p) w -> p kc w", p=128)[:, :, v0:b]
            dst = xT[:, :, v0 - base:b - base]
            T.dma("gpsimd", self.s_x, lambda e, dst=dst, src=src: e.dma_start(out=dst, in_=src),
                  writes=XT_ALL)

        def xcols(lo, hi):
            return slice(lo - base, hi - base)

        sched = [("u", j, 0) for j in range(16)]
        for j in range(16):
            sched += [("v", j, 0), ("B", j, 0), ("B", j, 1), ("B", j, 2), ("C", j, 0), ("C", j, 1)]
        assert len(sched) == 112
        bg = []

        def run_bg(k):
            for _ in range(min(k, len(bg))):
                bg.pop(0)()
        slot = view = None
        pend = {}
        deferred = []

        def flush_deferred(all_=False):
            while deferred:
                items = deferred[:]
                del deferred[:]
                for f in items:
                    f()
                if not all_:
                    break

        region_extra = self.region_init_res if (st == 0 and L == 0) else []

        for ci, (kind, j, sub) in enumerate(sched):
            if ci % 2 == 0:
                slot, view = self.load_slab(L, off, 8192)
                off += 8192
                view3 = view.rearrange("p (kc n) -> p kc n", kc=32)
            co = (ci % 2) * 128
            if kind == "u":
                lo, hi = a, b
            elif kind == "v":
                lo, hi = v0, b
            else:
                lo, hi = c0, b
            n = hi - lo
            pb = self.next_ps()
            ps = self.psm[pb][:, 0:n]
            pairs = [(view3[:, kc, co:co + 128], xT[:, kc, xcols(lo, hi)]) for kc in range(32)]
            self.mm_group(ps, pairs, reads=[("ring", slot)] + XT_ALL, writes=[("ps", pb)])
            flush_deferred()

            if kind == "u":
                T.op("scalar", lambda e, ps=ps, j=j, n=n: e.activation(
                    out=self.ug[:, j, 0:n], in_=ps, func=AF.Gelu_apprx_tanh),
                    reads=[("ps", pb)], writes=[("ug", j)] + region_extra)
            elif kind == "v":
                self.emit_v_epilogue(L, j, pb, ps, g, deferred)
            elif kind == "B":
                pend[sub] = (pb, ps)
                if sub == 2:
                    self.emit_B_epilogue(L, st, j, pend, g)
                    pend = {}
            elif kind == "C":
                pend[sub] = (pb, ps)
                if sub == 1:
                    self.emit_C_epilogue(L, st, j, pend, g, deferred, bg)
                    pend = {}
            run_bg(6)
        run_bg(len(bg))
        flush_deferred(all_=True)
        self.emit_C_finish(L, g)
        if self.dbg and self.dbg["stage"] == "s1" and self.dbg["L"] == L and self.dbg["st"] == st:
            self.emit_dbg_s1(Nm)
            return

        for d in range(32):
            slotY, vY = self.load_slab(L, off + 6144, 8192)
            slotZ, vZ = self.load_slab(L, off + 6144 + 8192, 4096)
            slotX, vX = self.load_slab(L, off, 6144)
            off += 6144 + 8192 + 4096
            vX3 = vX.rearrange("p (kc n) -> p kc n", kc=16)
            vY3 = vY.rearrange("p (kc n) -> p kc n", kc=32)
            vZ3 = vZ.rearrange("p (kc n) -> p kc n", kc=32)
            gts = []
            for i in range(3):
                pb = self.next_ps()
                ps = self.psm[pb][:, 0:Nm]
                if i < 2:
                    pairs = [(vY3[:, kc, i * 128:(i + 1) * 128], xT[:, kc, xcols(a, b)]) for kc in range(32)]
                    rs = [("ring", slotY)]
                else:
                    pairs = [(vZ3[:, kc, 0:128], xT[:, kc, xcols(a, b)]) for kc in range(32)]
                    rs = [("ring", slotZ)]
                self.mm_group(ps, pairs, reads=rs + XT_ALL, writes=[("ps", pb)])
                gt = tmp[i][:, 0:Nm]
                T.op("scalar", lambda e, gt=gt, ps=ps, i=i, d=d: e.activation(
                    out=gt, in_=ps, func=AF.Sigmoid, bias=self.pidx(L, "bg", i * 32 + d)),
                    reads=[("ps", pb), "par"], writes=[("tmp", i)])
                gts.append(gt)
            srcs = [(self.ug, "ug"), (self.bbr, "bbr"), (self.cc, "cc")]
            for i in range(3):
                pb = self.next_ps()
                ps = self.psm[pb][:, 0:Nm]
                buf, nm = srcs[i]
                pairs = [(vX3[:, kc, i * 128:(i + 1) * 128], buf[:, kc, 0:Nm]) for kc in range(16)]
                self.mm_group(ps, pairs, reads=[("ring", slotX)] + [(nm, k) for k in range(16)],
                              writes=[("ps", pb)])
                ti = tmp[3 + i][:, 0:Nm]
                T.op("vector", lambda e, ti=ti, gt=gts[i], ps=ps: e.tensor_tensor(
                    out=ti, in0=gt, in1=ps, op=ALU.mult),
                    reads=[("tmp", i), ("ps", pb)], writes=[("tmp", 3 + i)])
            t0, t1, t2 = tmp[3][:, 0:Nm], tmp[4][:, 0:Nm], tmp[5][:, 0:Nm]
            T.op("vector", lambda e, t0=t0, t1=t1: e.tensor_tensor(out=t0, in0=t0, in1=t1, op=ALU.add),
                 reads=[("tmp", 3), ("tmp", 4)], writes=[("tmp", 3)])
            T.op("vector", lambda e, t0=t0, t2=t2, d=d: e.tensor_tensor(
                out=self.mbuf[:, d, 0:Nm], in0=t0, in1=t2, op=ALU.add),
                reads=[("tmp", 3), ("tmp", 5)], writes=[("m", d), ("convC", d // 2)])

        if L == 0:
            res_src = lambda d: self.xw[d * 128:(d + 1) * 128, a:b]
        else:
            res_src = lambda d: self.x1scr[d, :, a:b]

        def s3_groups():
            nonlocal off
            for d2 in range(16):
                slot_, v_ = self.load_slab(L, off, 8192)
                off += 8192
                v3 = v_.rearrange("p (kc n) -> p kc n", kc=32)
                for hh in range(2):
                    d = d2 * 2 + hh
                    pb = self.next_ps()
                    ps = self.psm[pb][:, 0:Nm]
                    pairs = [(v3[:, kc, hh * 128:(hh + 1) * 128], self.mbuf[:, kc, 0:Nm]) for kc in range(32)]
                    self.mm_group(ps, pairs, reads=[("ring", slot_)] + [("m", k) for k in range(32)],
                                  writes=[("ps", pb)])
                    yield d, pb, ps

        def ln1_dst(d, ybuf, n):
            return self.xmscr[d, :, 0:n], []

        self.emit_ln(L, "l1g", "l1b", s3_groups(), res_src, Nm, xcols(a, b), ln1_dst, write_xT=True,
                     res_reads=(lambda d: [("x1", d)]) if L > 0 else (lambda d: []),
                     dst_writes=lambda d: [("xmscr", d)], region_free=[])

        o = f0 - a
        for c in range(NFC):
            slot_, v_ = self.load_slab(L, off, 8192)
            off += 8192
            v3 = v_.rearrange("p (kc n) -> p kc n", kc=32)
            hs = []
            for hh in range(2):
                cc_i = hh * NFC + c
                pb = self.next_ps()
                ps = self.psm[pb][:, 0:Nm]
                pairs = [(v3[:, kc, hh * 128:(hh + 1) * 128], xT[:, kc, xcols(a, b)]) for kc in range(32)]
                self.mm_group(ps, pairs, reads=[("ring", slot_)] + XT_ALL, writes=[("ps", pb)])
                ub = tmp[hh]
                to = (L * 172 + cc_i) * 2
                if st > 0:
                    T.op("scalar", lambda e, ub=ub, to=to: e.activation(
                        out=ub[:, 0:2], in_=self.utail[:, to:to + 2], func=AF.Copy),
                        reads=[("utail", L, cc_i)], writes=[("tmp", hh)])
                T.op("scalar", lambda e, ub=ub, ps=ps: e.activation(out=ub[:, 2:2 + Nm], in_=ps, func=AF.Copy),
                     reads=[("ps", pb)], writes=[("tmp", hh)])
                if st == 0:
                    nmk = HALO - a
                    T.op("vector", lambda e, ub=ub, nmk=nmk: e.tensor_scalar(
                        out=ub[:, 2:2 + nmk], in0=ub[:, 2:2 + nmk], scalar1=self.hmask[:, 0:1], scalar2=None,
                        op0=ALU.mult), reads=[("tmp", hh), "hmask"], writes=[("tmp", hh)])
                if st < self.n_st - 1:
                    T.op("scalar", lambda e, ub=ub, to=to: e.activation(
                        out=self.utail[:, to:to + 2], in_=ub[:, Nm:Nm + 2], func=AF.Copy),
                        reads=[("tmp", hh)], writes=[("utail", L, cc_i)])
                hb_ = tmp[2 + hh][:, 0:Nf]
                w = lambda k, cc_i=cc_i: self.pidx(L, "fcv", cc_i * 3 + k)
                T.op("vector", lambda e, hb_=hb_, ub=ub, w=w: e.tensor_scalar(
                    out=hb_, in0=ub[:, o + 2:o + 2 + Nf], scalar1=w(2), scalar2=None, op0=ALU.mult),
                    reads=[("tmp", hh), "par"], writes=[("tmp", 2 + hh)])
                for k in (1, 0):
                    T.op("vector", lambda e, hb_=hb_, ub=ub, w=w, k=k: e.scalar_tensor_tensor(
                        out=hb_, in0=ub[:, o + k:o + k + Nf], scalar=w(k), in1=hb_, op0=ALU.mult, op1=ALU.add),
                        reads=[("tmp", hh), ("tmp", 2 + hh), "par"], writes=[("tmp", 2 + hh)])
                hs.append(hb_)
            sl = tmp[4][:, 0:Nf]
            T.op("scalar", lambda e, sl=sl, h1=hs[0]: e.activation(out=sl, in_=h1, func=AF.Silu),
                 reads=[("tmp", 2)], writes=[("tmp", 4)])
            T.op("vector", lambda e, sl=sl, h2=hs[1], c=c: e.tensor_tensor(
                out=self.act[:, c, 0:Nf], in0=sl, in1=h2, op=ALU.mult),
                reads=[("tmp", 4), ("tmp", 3)], writes=[("act", c)])

        res_src2 = lambda d: self.xmscr[d, :, o:o + Nf]

        def s5_groups():
            nonlocal off
            for d in range(32):
                pb = self.next_ps()
                ps = self.psm[pb][:, 0:Nf]
                for hh in range(2):
                    slot_, v_ = self.load_slab(L, off, 5504)
                    off += 5504
                    v3 = v_.rearrange("p (kc n) -> p kc n", kc=43)
                    pairs = [(v3[:, kc, :], self.act[:, hh * 43 + kc, 0:Nf]) for kc in range(43)]
                    self.mm_group(ps, pairs, reads=[("ring", slot_)] + [("act", hh * 43 + k) for k in range(43)],
                                  writes=[("ps", pb)], first=(hh == 0), last=(hh == 1))
                yield d, pb, ps

        if last_layer:
            def ln2_dst(d, ybuf, n):
                return self.out_d[d * 128:(d + 1) * 128, f0 - HALO:b - HALO], []
            dstw = lambda d: [("out", d, st)]
        else:
            def ln2_dst(d, ybuf, n):
                return self.x1scr[d, :, f0:b], []
            dstw = lambda d: [("x1", d)]
        self.emit_ln(L, "l2g", "l2b", s5_groups(), res_src2, Nf, xcols(f0, b), ln2_dst,
                     write_xT=(not last_layer), res_reads=lambda d: [("xmscr", d)], dst_writes=dstw, region_free=[],
                     final=last_layer)
        assert off == PER_LAYER, (off, PER_LAYER)

    def emit_ln(self, L, gname, bname, groups, res_src, n, xsl, dst_fn, write_xT, res_reads, dst_writes,
                region_free, final=False):
        T = self.T
        tmp = self.tmp
        prev = None

        def stats(d, rb_i, sq_i):
            rt = self.stbuf[rb_i][:, 0:n]
            sq = tmp[6 + sq_i][:, 0:n]
            self.T.group("tensor", [
                lambda e, rt=rt, d=d: e.matmul(self.psS1[:, 0:n], self.onesD[:], rt, start=(d == 0), stop=(d == 31)),
                lambda e, sq=sq, d=d: e.matmul(self.psS2[:, 0:n], self.onesD[:], sq, start=(d == 0), stop=(d == 31)),
            ], reads=[("stbuf", rb_i), ("tmp", 6 + sq_i), "onesD"], writes=["psS"])

        for d, pb, ps in groups:
            if prev is not None:
                stats(*prev)
            li = self.ld_i % self.NLD
            self.ld_i += 1
            lb = self.ldbuf[li][:, 0:n]
            src = res_src(d)
            rr = res_reads(d)
            T.dma("sync", self.s_ld[li], lambda e, lb=lb, src=src: e.dma_start(out=lb, in_=src),
                  reads=rr, writes=[("ldbuf", li)])
            si = self.st_i % self.NSTB
            self.st_i += 1
            rt = self.stbuf[si][:, 0:n]
            T.op("vector", lambda e, rt=rt, lb=lb, ps=ps: e.scalar_tensor_tensor(
                out=rt, in0=lb, scalar=float(ALPHA), in1=ps, op0=ALU.mult, op1=ALU.add),
                reads=[("ldbuf", li), ("ps", pb)], writes=[("stbuf", si)])
            sq_i = d % 2
            sq = tmp[6 + sq_i][:, 0:n]
            T.op("scalar", lambda e, sq=sq, rt=rt: e.activation(out=sq, in_=rt, func=AF.Square),
                 reads=[("stbuf", si)], writes=[("tmp", 6 + sq_i)])
            dst = self.rscr[d, :, 0:n]
            T.dma("sync", self.s_st[si], lambda e, dst=dst, rt=rt: e.dma_start(out=dst, in_=rt),
                  reads=[("stbuf", si)], writes=[("rscr", d)])
            prev = (d, si, sq_i)
        stats(*prev)
        mean = tmp[8][:, 0:n]
        rstd = tmp[9][:, 0:n]
        self.finish_stats(n, mean, rstd, 8, 9)
        for d in range(32):
            li = self.ld_i % self.NLD
            self.ld_i += 1
            lb = self.ldbuf[li][:, 0:n]
            src = self.rscr[d, :, 0:n]
            T.dma("sync", self.s_ld[li], lambda e, lb=lb, src=src: e.dma_start(out=lb, in_=src),
                  reads=[("rscr", d)], writes=[("ldbuf", li)])
            T.op("vector", lambda e, lb=lb, rstd=rstd: e.tensor_tensor(out=lb, in0=lb, in1=rstd, op=ALU.mult),
                 reads=[("ldbuf", li), ("tmp", 9)], writes=[("ldbuf", li)])
            T.op("vector", lambda e, lb=lb, mean=mean: e.tensor_tensor(out=lb, in0=lb, in1=mean, op=ALU.add),
                 reads=[("ldbuf", li), ("tmp", 8)], writes=[("ldbuf", li)])
            si = self.st_i % self.NSTB
            self.st_i += 1
            yt = self.stbuf[si][:, 0:n]
            gi = self.pidx(L, gname, d)
            bi = self.pidx(L, bname, d)
            T.op("scalar", lambda e, yt=yt, lb=lb, gi=gi, bi=bi: e.activation(
                out=yt, in_=lb, func=AF.Identity, scale=gi, bias=bi),
                reads=[("ldbuf", li), "par"], writes=[("stbuf", si)])
            if write_xT:
                T.op("scalar", lambda e, lb=lb, d=d, gi=gi, bi=bi: e.activation(
                    out=self.xT[:, d, xsl], in_=lb, func=AF.Identity, scale=gi, bias=bi),
                    reads=[("ldbuf", li), "par"], writes=[("xT", d)])
            dst, _ = dst_fn(d, yt, n)
            dw = dst_writes(d)
            sem = self.s_out if final else self.s_st[si]
            T.dma("sync", sem, lambda e, dst=dst, yt=yt: e.dma_start(out=dst, in_=yt),
                  reads=[("stbuf", si)], writes=dw)
            if final:
                self.final_res += dw

    def finish_stats(self, n, mean, rstd, mi, ri, S1=None, S2=None, sres=("psS",)):
        T = self.T
        S1 = self.psS1[:, 0:n] if S1 is None else S1
        S2 = self.psS2[:, 0:n] if S2 is None else S2
        sres = list(sres)
        T.op("scalar", lambda e: e.activation(out=mean, in_=S1, func=AF.Copy),
             reads=sres, writes=[("tmp", mi)])
        T.op("vector", lambda e: e.tensor_tensor(out=rstd, in0=mean, in1=mean, op=ALU.mult),
             reads=[("tmp", mi)], writes=[("tmp", ri)])
        T.op("vector", lambda e: e.tensor_tensor(out=rstd, in0=S2, in1=rstd, op=ALU.subtract),
             reads=sres + [("tmp", ri)], writes=[("tmp", ri)])
        T.op("scalar", lambda e: e.activation(out=rstd, in_=rstd, func=AF.Sqrt, bias=float(LN_EPS)),
             reads=[("tmp", ri)], writes=[("tmp", ri)])
        T.op("vector", lambda e: e.reciprocal(out=rstd, in_=rstd),
             reads=[("tmp", ri)], writes=[("tmp", ri)])
        T.op("vector", lambda e: e.scalar_tensor_tensor(out=mean, in0=mean, scalar=-1.0, in1=rstd,
                                                        op0=ALU.mult, op1=ALU.mult),
             reads=[("tmp", mi), ("tmp", ri)], writes=[("tmp", mi)])

    def emit_v_epilogue(self, L, h, pb, ps, g, deferred):
        T = self.T
        tmp = self.tmp
        v0, a, b = g["v0"], g["a"], g["b"]
        Nv, Nm = b - v0, b - a
        vg = tmp[0][:, 0:Nv]
        vsq = tmp[1][:, 0:Nv]
        bk = self.bs_i % 2
        self.bs_i += 1
        wo_ = (L * 16 + h) * 128
        T.dma("sync", self.s_bs[bk], lambda e: e.dma_start(out=self.bsr[bk][:], in_=self.abs_d[0:1, wo_:wo_ + 128]),
              writes=[("bsr", bk)])
        T.op("scalar", lambda e: e.activation(out=vg, in_=ps, func=AF.Gelu_apprx_tanh),
             reads=[("ps", pb)], writes=[("tmp", 0)])
        T.op("scalar", lambda e: e.activation(out=vsq, in_=vg, func=AF.Square),
             reads=[("tmp", 0)], writes=[("tmp", 1)])

        def pe_stats():
            S1 = self.psG[:, 0:Nv]
            S2 = self.psX[:, 0:Nv]
            T.group("tensor", [lambda e: e.matmul(S1, self.onesV[:], vg, start=True, stop=True)],
                    reads=[("tmp", 0), "onesV"], writes=["psG"])
            T.group("tensor", [lambda e: e.matmul(S2, self.onesV[:], vsq, start=True, stop=True)],
                    reads=[("tmp", 1), "onesV"], writes=["psT"])
            mean = tmp[2][:, 0:Nv]
            rstd = tmp[3][:, 0:Nv]
            self.finish_stats(Nv, mean, rstd, 2, 3, S1, S2, sres=["psG", "psT"])
            T.op("vector", lambda e: e.tensor_tensor(out=vg, in0=vg, in1=rstd, op=ALU.mult),
                 reads=[("tmp", 0), ("tmp", 3)], writes=[("tmp", 0)])
            T.op("vector", lambda e: e.tensor_tensor(out=vg, in0=vg, in1=mean, op=ALU.add),
                 reads=[("tmp", 0), ("tmp", 2)], writes=[("tmp", 0)])
            vln = self.tmpb[0][:, 0:Nv]
            T.op("scalar", lambda e: e.activation(
                out=vln, in_=vg, func=AF.Identity, scale=self.pidx(L, "alg", h), bias=self.pidx(L, "alb", h)),
                reads=[("tmp", 0), "par"], writes=[("tmpb", 0)])
            deferred.append(pe_tr)

        def pe_tr():
            nq = Nv // 128
            T.group("tensor", [
                (lambda e, q=q: e.transpose(self.psT[:, q * 128:(q + 1) * 128],
                                            self.tmpb[0][:, q * 128:(q + 1) * 128], self.ident[:]))
                for q in range(nq)][:], reads=[("tmpb", 0), "ident"], writes=["psT"])
            vT = self.tmpb[1][:, 0:Nv]
            T.op("scalar", lambda e: e.activation(out=vT, in_=self.psT[:, 0:Nv], func=AF.Copy),
                 reads=["psT"], writes=[("tmpb", 1)])
            deferred.append(pe_sgu)

        def pe_sgu():
            nq = Nv // 128
            wo = (L * 16 + h) * 128
            fns = []
            for q in range(nq):
                fns.append(lambda e, q=q: e.matmul(self.psG[:, q * 128:(q + 1) * 128],
                                                   self.tmpb[1][:, q * 128:(q + 1) * 128],
                                                   self.WT[:, wo:wo + 128], start=True, stop=False))
                fns.append(lambda e, q=q: e.matmul(self.psG[:, q * 128:(q + 1) * 128],
                                                   self.onesrow[0:1, :],
                                                   self.bsr[bk][0:1, :], start=False, stop=True))
            T.group("tensor", fns, reads=[("tmpb", 1), ("WT", L, h), "onesrow", ("bsr", bk)], writes=["psG"])
            sk = a - v0
            T.op("vector", lambda e: e.tensor_tensor(
                out=self.ug[:, h, 0:Nm], in0=self.ug[:, h, 0:Nm], in1=self.psG[:, sk:sk + Nm], op=ALU.mult),
                reads=[("ug", h), "psG"], writes=[("ug", h)])

        deferred.append(pe_stats)

    def emit_B_epilogue(self, L, st, j, pend, g):
        T = self.T
        tmp = self.tmp
        c0, a, b = g["c0"], g["a"], g["b"]
        Nc, Nm = b - c0, b - a
        o = a - c0
        (pgc, psgc), (phb, pshb), (pgb, psgb) = pend[0], pend[1], pend[2]
        hb = tmp[4][:, 0:Nc]
        T.op("scalar", lambda e: e.activation(out=hb, in_=pshb, func=AF.Copy),
             reads=[("ps", phb)], writes=[("tmp", 4)])
        qb = tmp[5]
        to = (L * 16 + j) * 2
        wr = [("tmp", 5)]
        if st > 0:
            T.op("scalar", lambda e: e.activation(out=qb[:, 0:2], in_=self.qtail[:, to:to + 2], func=AF.Copy),
                 reads=[("qtail", L, j)], writes=wr)
        T.op("vector", lambda e: e.tensor_tensor(out=qb[:, 2:2 + Nc], in0=psgc, in1=hb, op=ALU.mult),
             reads=[("ps", pgc), ("tmp", 4), ("tmp", 5)], writes=wr)
        if st == 0:
            nmk = HALO - c0
            T.op("vector", lambda e: e.tensor_scalar(
                out=qb[:, 2:2 + nmk], in0=qb[:, 2:2 + nmk], scalar1=self.hmask[:, 0:1], scalar2=None,
                op0=ALU.mult), reads=[("tmp", 5), "hmask"], writes=wr)
        if st < self.n_st - 1:
            T.op("scalar", lambda e: e.activation(out=self.qtail[:, to:to + 2], in_=qb[:, Nc:Nc + 2], func=AF.Copy),
                 reads=[("tmp", 5)], writes=[("qtail", L, j)])
        acc = tmp[6][:, 0:Nm]
        w = lambda k: self.pidx(L, "bcv", j * 3 + k)
        T.op("vector", lambda e: e.tensor_scalar(out=acc, in0=qb[:, o + 2:o + 2 + Nm], scalar1=w(2), scalar2=None,
                                                 op0=ALU.mult),
             reads=[("tmp", 5), "par"], writes=[("tmp", 6)])
        for k in (1, 0):
            T.op("vector", lambda e, k=k: e.scalar_tensor_tensor(
                out=acc, in0=qb[:, o + k:o + k + Nm], scalar=w(k), in1=acc, op0=ALU.mult, op1=ALU.add),
                reads=[("tmp", 5), ("tmp", 6), "par"], writes=[("tmp", 6)])
        T.op("vector", lambda e: e.tensor_tensor(out=self.bbr[:, j, 0:Nm], in0=acc, in1=psgb[:, o:o + Nm],
                                                 op=ALU.mult),
             reads=[("tmp", 6), ("ps", pgb)], writes=[("bbr", j)])

    def emit_C_epilogue(self, L, st, j, pend, g, deferred, bg):
        T = self.T
        tmp = self.tmp
        c0, a, b = g["c0"], g["a"], g["b"]
        Nc, Nm = b - c0, b - a
        o = a - c0
        (pa, psa), (pg, psg) = pend[0], pend[1]
        sg = tmp[4][:, 0:Nc]
        T.op("scalar", lambda e: e.activation(out=sg, in_=psg, func=AF.Sigmoid),
             reads=[("ps", pg)], writes=[("tmp", 4)])
        cbi = 7 if j % 2 == 0 else 10
        cb = tmp[cbi]
        to = (L * 16 + j) * 30
        wr = [("tmp", cbi)]
        if st > 0:
            T.op("scalar", lambda e: e.activation(out=cb[:, 0:30], in_=self.catail[:, to:to + 30], func=AF.Copy),
                 reads=[("catail", L, j)], writes=wr)
        T.op("vector", lambda e: e.tensor_tensor(out=cb[:, 30:30 + Nc], in0=psa, in1=sg, op=ALU.mult),
             reads=[("ps", pa), ("tmp", 4), ("tmp", cbi)], writes=wr)
        if st == 0:
            nmk = HALO - c0
            T.op("vector", lambda e: e.tensor_scalar(
                out=cb[:, 30:30 + nmk], in0=cb[:, 30:30 + nmk], scalar1=self.hmask[:, 0:1], scalar2=None,
                op0=ALU.mult), reads=[("tmp", cbi), "hmask"], writes=wr)
        if st < self.n_st - 1:
            T.op("scalar", lambda e: e.activation(out=self.catail[:, to:to + 30], in_=cb[:, Nc:Nc + 30],
                                                  func=AF.Copy),
                 reads=[("tmp", cbi)], writes=[("catail", L, j)])
        acc = self.convC[:, j, 0:Nm]
        w = lambda k: self.pidx(L, "ccv", j * 31 + k)
        T.op("vector", lambda e: e.tensor_scalar(
            out=acc, in0=cb[:, o + 30:o + 30 + Nm], scalar1=w(30), scalar2=self.pidx(L, "ccb", j),
            op0=ALU.mult, op1=ALU.add), reads=[("tmp", cbi), "par"],
            writes=[("convC", j), ("m", 2 * j), ("m", 2 * j + 1)])
        for k in range(29, -1, -1):
            bg.append(lambda k=k: T.op("vector", lambda e: e.scalar_tensor_tensor(
                out=acc, in0=cb[:, o + k:o + k + Nm], scalar=w(k), in1=acc, op0=ALU.mult, op1=ALU.add),
                reads=[("tmp", cbi), ("convC", j), "par"], writes=[("convC", j)]))
        sq = tmp[8 + (j % 2)][:, 0:Nm]

        def pe_stats():
            T.group("tensor", [
                lambda e: e.matmul(self.psS1[:, 0:Nm], self.onesC[:], acc, start=(j == 0), stop=(j == 15)),
                lambda e: e.matmul(self.psS2[:, 0:Nm], self.onesC[:], sq, start=(j == 0), stop=(j == 15)),
            ], reads=[("convC", j), ("tmp", 8 + (j % 2)), "onesC"], writes=["psS"])

        def fin():
            T.op("scalar", lambda e: e.activation(out=sq, in_=acc, func=AF.Square),
                 reads=[("convC", j)], writes=[("tmp", 8 + (j % 2))])
            deferred.append(pe_stats)
        bg.append(fin)

    def emit_C_finish(self, L, g):
        T = self.T
        tmp = self.tmp
        a, b = g["a"], g["b"]
        Nm = b - a
        mean = tmp[0][:, 0:Nm]
        rstd = tmp[1][:, 0:Nm]
        self.finish_stats(Nm, mean, rstd, 0, 1)
        for j in range(16):
            t = tmp[2 + (j % 2)][:, 0:Nm]
            acc = self.convC[:, j, 0:Nm]
            T.op("vector", lambda e, t=t, acc=acc: e.tensor_tensor(out=t, in0=acc, in1=rstd, op=ALU.mult),
                 reads=[("convC", j), ("tmp", 1)], writes=[("tmp", 2 + (j % 2))])
            T.op("vector", lambda e, t=t: e.tensor_tensor(out=t, in0=t, in1=mean, op=ALU.add),
                 reads=[("tmp", 2 + (j % 2)), ("tmp", 0)], writes=[("tmp", 2 + (j % 2))])
            T.op("scalar", lambda e, t=t, j=j: e.activation(
                out=self.cc[:, j, 0:Nm], in_=t, func=AF.Silu, scale=self.pidx(L, "clg", j),
                bias=self.pidx(L, "clb", j)),
                reads=[("tmp", 2 + (j % 2)), "par"], writes=[("cc", j)])

    def emit_dbg_s1(self, Nm):
        T = self.T
        s = self.s_out
        for i, (buf, nm) in enumerate(((self.ug, "ug"), (self.bbr, "bbr"), (self.cc, "cc"))):
            for j in range(16):
                t = self.tmp[j % 4][:, 0:Nm]
                T.op("vector", lambda e, t=t, buf=buf, j=j: e.tensor_copy(out=t, in_=buf[:, j, 0:Nm]),
                     reads=[(nm, j)], writes=[("tmp", j % 4)])
                dst = self.dbg_d[i, j, :, 0:Nm]
                T.dma("sync", s, lambda e, dst=dst, t=t: e.dma_start(out=dst, in_=t),
                      reads=[("tmp", j % 4)], writes=[("dbg", i, j)])
                self.final_res.append(("dbg", i, j))


def _tile(W, kc):
    n = W.shape[1]
    return np.ascontiguousarray(W.reshape(kc, 128, n).transpose(1, 0, 2)).reshape(128, kc * n)


def _vec(v):
    return np.ascontiguousarray(v.reshape(-1, 128).T)


def build_stream(inp, L):
    w_in = inp["w_in"][L]
    parts = []
    cols = []
    for j in range(16):
        cols.append(OFF_A + j * 128)
    for j in range(16):
        cols.append(OFF_A + DA + j * 128)
        cols += [OFF_B + DA + j * 128, OFF_B + 2 * DA + j * 128, OFF_B + j * 128]
        cols += [OFF_C + j * 128, OFF_C + DA + j * 128]
    for i in range(0, 112, 2):
        blk = np.concatenate([w_in[:, cols[i]:cols[i] + 128], w_in[:, cols[i + 1]:cols[i + 1] + 128]], axis=1)
        parts.append(_tile(blk, 32))
    a_out, b_out, c_out = inp["a_out"][L], inp["b_out"][L], inp["c_out"][L]
    for d in range(32):
        sl = slice(d * 128, (d + 1) * 128)
        parts.append(_tile(np.concatenate([a_out[:, sl], b_out[:, sl], c_out[:, sl]], axis=1), 16))
        g0 = OFF_G + d * 128
        parts.append(_tile(np.concatenate([w_in[:, g0:g0 + 128], w_in[:, g0 + D:g0 + D + 128]], axis=1), 32))
        parts.append(_tile(w_in[:, g0 + 2 * D:g0 + 2 * D + 128], 32))
    w_o = inp["w_o"][L]
    for d2 in range(16):
        parts.append(_tile(w_o[:, d2 * 256:(d2 + 1) * 256], 32))
    f_up = inp["f_up"][L]
    for c in range(NFC):
        parts.append(_tile(np.concatenate([f_up[:, c * 128:(c + 1) * 128],
                                           f_up[:, DFF + c * 128:DFF + (c + 1) * 128]], axis=1), 32))
    f_down = inp["f_down"][L]
    for d in range(32):
        for hh in range(2):
            parts.append(_tile(f_down[hh * 43 * 128:(hh + 1) * 43 * 128, d * 128:(d + 1) * 128], 43))
    out = np.concatenate(parts, axis=1)
    assert out.shape == (128, PER_LAYER), out.shape
    return out


def build_par(inp):
    par = np.zeros((128, NPAR), np.float32)
    for L in range(DEPTH):
        o = L * NPL

        def put(name, arr):
            par[:, o + PO[name]:o + PO[name] + arr.shape[1]] = arr
        put("bg", np.concatenate([_vec(inp["b_gate"][L, i]) for i in range(3)], axis=1))
        put("alg", _vec(inp["a_ln_g"][L]))
        put("alb", _vec(inp["a_ln_b"][L]))
        bc = inp["b_conv"][L]
        put("bcv", np.stack([_vec(bc[k]) for k in range(3)], axis=2).reshape(128, 48))
        ccv = inp["c_conv"][L]
        put("ccv", np.stack([_vec(ccv[k]) for k in range(31)], axis=2).reshape(128, 496))
        put("ccb", _vec(inp["c_conv_b"][L]))
        put("clg", _vec(inp["c_ln_g"][L]))
        put("clb", _vec(inp["c_ln_b"][L]))
        put("l1g", _vec(inp["ln1_g"][L]))
        put("l1b", _vec(inp["ln1_b"][L]))
        put("l2g", _vec(inp["ln2_g"][L]))
        put("l2b", _vec(inp["ln2_b"][L]))
        fc = inp["f_conv"][L]
        put("fcv", np.stack([_vec(fc[k]) for k in range(3)], axis=2).reshape(128, 516))
    return par


def host_prepare(inp):
    x = inp["x"]
    shared = {}
    shared["wst"] = np.concatenate([build_stream(inp, L) for L in range(DEPTH)], axis=1)
    shared["par"] = build_par(inp)
    shared["awst"] = np.ascontiguousarray(inp["a_ws"].transpose(3, 0, 1, 2)).reshape(128, DEPTH * 2048)
    cmask = np.triu(np.ones((128, 128), np.float32))
    shared["cst"] = np.concatenate([cmask, np.eye(128, dtype=np.float32)], axis=1)
    shared["absr"] = np.ascontiguousarray(inp["a_bs"].reshape(1, DEPTH * 2048))
    per_core = []
    for c in range(NCORES):
        bi, q = c // 4, c % 4
        s = q * NOUT
        xw = np.zeros((D, NWIN), np.float32)
        lo = max(0, s - HALO)
        xw[:, lo - (s - HALO):] = x[bi, lo:s + NOUT, :].T
        hm = np.full((128, 1), 0.0 if q == 0 else 1.0, np.float32)
        m = dict(shared)
        m["xw"] = xw
        m["hmask"] = hm
        per_core.append(m)
    return per_core


_NC_CACHE = {}


def kernel(**inputs):
    inp = {k: np.asarray(v) for k, v in inputs.items()}
    in_maps = host_prepare(inp)
    if "nc" not in _NC_CACHE:
        _NC_CACHE["nc"] = Builder().build()
    nc = _NC_CACHE["nc"]
    res = run_bass_kernel_spmd(nc, in_maps, core_ids=list(range(NCORES)))
    out = np.empty((2, 4096, D), np.float32)
    for c in range(NCORES):
        bi, q = c // 4, c % 4
        out[bi, q * NOUT:(q + 1) * NOUT, :] = res.results[c]["out"].T
    return out
```
